# Optimizing a Trainium2 kernel written in Bass

```python
import jax, jax.numpy as jnp
from jax import lax
import numpy as np

D_MODEL = 2048
BATCH = 2
SEQ = 4096
DEPTH = 4
DEC_BATCH = 8
DEC_SEQ = 32
PAST_LEN = 4096

CHUNK = 64
N_MIXERS = 2
N_MLSTM = (DEPTH + 1) // 2
N_FOX = DEPTH // 2
PLE_DIM = 256
D_FF = 5632
FFN_HALF = 0.5
NORM_EPS = 1e-6
M_HEADS = 8
M_DQK = D_MODEL // (2 * M_HEADS)
M_DV = D_MODEL // M_HEADS
M_QK = M_HEADS * M_DQK
M_V = M_HEADS * M_DV
M_IN = 2 * M_QK + 2 * M_V + 2 * M_HEADS
F_HEADS = 16
F_HD = D_MODEL // F_HEADS
F_IN = 4 * D_MODEL + F_HEADS
Q_BLOCK = 128

kernel_name = 'hybrid_mlstm_fox_streaming_step'


def rmsnorm(x, g):
    xf = x.astype(jnp.float32)
    y = xf * lax.rsqrt(jnp.mean(xf * xf, axis=-1, keepdims=True) + NORM_EPS)
    return (y * g.astype(jnp.float32)).astype(x.dtype)


def swiglu(x, w_in, w_out):
    g, u = jnp.split(x @ w_in, 2, axis=-1)
    return (jax.nn.silu(g) * u) @ w_out


def mlstm_recurrence(q, k, v, ig, lf, C0, n0, m0, block):
    B, T, H, _ = q.shape
    nc = T // block

    def to_blocks(a):
        a = a.reshape((B, nc, block) + a.shape[2:])
        return jnp.swapaxes(jnp.moveaxis(a, 1, 0), 2, 3)

    causal = jnp.tril(jnp.ones((block, block), bool))

    def step(carry, blk):
        C, n, m = carry
        qb, kb, vb, ib, fb = blk
        b = jnp.cumsum(fb, axis=-1)
        dmat = jnp.where(causal, b[..., :, None] - b[..., None, :] + ib[..., None, :], -jnp.inf)
        g = b + m[..., None]
        m_t = jnp.maximum(g, jnp.max(dmat, axis=-1))
        s = jnp.einsum('bhtd,bhsd->bhts', qb, kb) * jnp.exp(dmat - m_t[..., None])
        inter = jnp.exp(g - m_t)
        num = jnp.einsum('bhts,bhsv->bhtv', s, vb) + inter[..., None] * jnp.einsum('bhtd,bhdv->bhtv', qb, C)
        den = jnp.sum(s, axis=-1) + inter * jnp.einsum('bhtd,bhd->bht', qb, n)
        h = num / jnp.maximum(jnp.abs(den), jnp.exp(-m_t))[..., None]
        b_end = b[..., -1]
        m_new = m_t[..., -1]
        decay = jnp.exp(b_end + m - m_new)
        wa = jnp.exp(b_end[..., None] - b + ib - m_new[..., None])
        C_new = decay[..., None, None] * C + jnp.einsum('bhs,bhsd,bhsv->bhdv', wa, kb, vb)
        n_new = decay[..., None] * n + jnp.einsum('bhs,bhsd->bhd', wa, kb)
        return (C_new, n_new, m_new), h

    xs = (to_blocks(q), to_blocks(k), to_blocks(v), to_blocks(ig), to_blocks(lf))
    (C, n, m), h = lax.scan(step, (C0, n0, m0), xs)
    h = jnp.moveaxis(jnp.swapaxes(h, 2, 3), 0, 1).reshape(B, T, H, M_DV)
    return h, C, n, m


def mlstm_mixer(xn, w_in, b_gates, g_h, w_out, C0, n0, m0, block):
    B, T, _ = xn.shape
    f32 = jnp.float32
    q, k, v, o, ig, fg = jnp.split(xn @ w_in, [M_QK, 2 * M_QK, 2 * M_QK + M_V, 2 * M_QK + 2 * M_V,
                                              2 * M_QK + 2 * M_V + M_HEADS], axis=-1)
    q = q.reshape(B, T, M_HEADS, M_DQK).astype(f32)
    k = k.reshape(B, T, M_HEADS, M_DQK).astype(f32) * (M_DQK ** -0.5)
    v = v.reshape(B, T, M_HEADS, M_DV).astype(f32)
    ig = ig.astype(f32) + b_gates[0].astype(f32)
    lf = jax.nn.log_sigmoid(fg.astype(f32) + b_gates[1].astype(f32))
    h, C, n, m = mlstm_recurrence(q, k, v, ig, lf, C0.astype(f32), n0.astype(f32), m0.astype(f32), block)
    h = rmsnorm(h, g_h.reshape(M_HEADS, M_DV)).reshape(B, T, M_V).astype(xn.dtype)
    return (h * jax.nn.sigmoid(o)) @ w_out, (C, n, m)


def fox_project(xn, w_in, b_f, g_qk):
    B, T, _ = xn.shape
    D = D_MODEL
    q, k, v, og, fg = jnp.split(xn @ w_in, [D, 2 * D, 3 * D, 4 * D], axis=-1)
    q = rmsnorm(q.reshape(B, T, F_HEADS, F_HD), g_qk[0])
    k = rmsnorm(k.reshape(B, T, F_HEADS, F_HD), g_qk[1])
    v = v.reshape(B, T, F_HEADS, F_HD)
    lf = jax.nn.log_sigmoid(fg.astype(jnp.float32) + b_f.astype(jnp.float32))
    return q, k, v, og, lf


def fox_attend(q, k, v, c_q, c_k, pos_q, pos_k):
    logits = jnp.einsum('bqhd,bkhd->bhqk', q, k).astype(jnp.float32) * (F_HD ** -0.5)
    logits = logits + jnp.swapaxes(c_q, 1, 2)[..., :, None] - jnp.swapaxes(c_k, 1, 2)[..., None, :]
    logits = jnp.where(pos_k[None, :] <= pos_q[:, None], logits, -jnp.inf)
    p = jax.nn.softmax(logits, axis=-1)
    return jnp.einsum('bhqk,bkhd->bqhd', p.astype(v.dtype), v)


def fox_prompt(xn, w_in, b_f, g_qk, w_out):
    B, T, _ = xn.shape
    q, k, v, og, lf = fox_project(xn, w_in, b_f, g_qk)
    c = jnp.cumsum(lf, axis=1)
    pos = jnp.arange(T)
    nb = T // Q_BLOCK
    qs = jnp.moveaxis(q.reshape(B, nb, Q_BLOCK, F_HEADS, F_HD), 1, 0)
    cs = jnp.moveaxis(c.reshape(B, nb, Q_BLOCK, F_HEADS), 1, 0)
    ps = pos.reshape(nb, Q_BLOCK)
    o = lax.map(lambda a: fox_attend(a[0], k, v, a[1], c, a[2], pos), (qs, cs, ps))
    o = jnp.moveaxis(o, 0, 1).reshape(B, T, D_MODEL)
    return (o * jax.nn.sigmoid(og)) @ w_out, (k, v, lf)


def fox_sample(xn, k_past, v_past, lf_past, w_in, b_f, g_qk, w_out):
    B, T, _ = xn.shape
    P = k_past.shape[1]
    q, k, v, og, lf = fox_project(xn, w_in, b_f, g_qk)
    lf_past = lf_past.astype(jnp.float32)
    c_past = jnp.cumsum(lf_past, axis=1) - jnp.sum(lf_past, axis=1, keepdims=True)
    c_new = jnp.cumsum(lf, axis=1)
    keys = jnp.concatenate([k_past.astype(k.dtype), k], axis=1)
    vals = jnp.concatenate([v_past.astype(v.dtype), v], axis=1)
    c_k = jnp.concatenate([c_past, c_new], axis=1)
    o = fox_attend(q, keys, vals, c_new, c_k, P + jnp.arange(T), jnp.arange(P + T))
    o = o.reshape(B, T, D_MODEL)
    return (o * jax.nn.sigmoid(og)) @ w_out, (k, v, lf)


def run_trunk(x, p, prm, mlstm_init, fox_past):
    B, T, _ = x.shape
    block = min(CHUNK, T)
    m_states, f_rows = [], []
    for i in range(DEPTH):
        g = prm['norm_gains'][i]
        x = x + FFN_HALF * swiglu(rmsnorm(x, g[0]), prm['ffn1_in'][i], prm['ffn1_out'][i])
        xn = rmsnorm(x, g[1])
        j = i // N_MIXERS
        if i % N_MIXERS == 0:
            if mlstm_init is None:
                C0 = jnp.zeros((B, M_HEADS, M_DQK, M_DV), jnp.float32)
                n0 = jnp.zeros((B, M_HEADS, M_DQK), jnp.float32)
                m0 = jnp.zeros((B, M_HEADS), jnp.float32)
            else:
                C0, n0, m0 = mlstm_init[0][j], mlstm_init[1][j], mlstm_init[2][j]
            y, st = mlstm_mixer(xn, prm['mlstm_w_in'][j], prm['mlstm_b_gates'][j], prm['mlstm_g_h'][j],
                                prm['mlstm_w_out'][j], C0, n0, m0, block)
            m_states.append(st)
        else:
            if fox_past is None:
                y, rows = fox_prompt(xn, prm['fox_w_in'][j], prm['fox_b_f'][j], prm['fox_g_qk'][j],
                                     prm['fox_w_out'][j])
            else:
                y, rows = fox_sample(xn, fox_past[0][j], fox_past[1][j], fox_past[2][j],
                                     prm['fox_w_in'][j], prm['fox_b_f'][j], prm['fox_g_qk'][j],
                                     prm['fox_w_out'][j])
            f_rows.append(rows)
        x = x + y
        x = x + FFN_HALF * swiglu(rmsnorm(x, g[2]), prm['ffn2_in'][i], prm['ffn2_out'][i])
        gate = jax.nn.sigmoid(rmsnorm(x, g[3]) @ prm['ple_gate'][i])
        x = x + gate * (p[i].astype(x.dtype) @ prm['ple_proj'][i])
    C = jnp.stack([s[0] for s in m_states])
    n = jnp.stack([s[1] for s in m_states])
    m = jnp.stack([s[2] for s in m_states])
    k = jnp.stack([r[0] for r in f_rows])
    v = jnp.stack([r[1] for r in f_rows])
    lf = jnp.stack([r[2] for r in f_rows])
    return x, C, n, m, k, v, lf


def setup_inputs(seed: int = 0) -> dict:
    key = jax.random.key(seed)
    ks = jax.random.split(key, 32)
    f32 = jnp.float32
    D = D_MODEL

    def nrm(k, shape, scale):
        return jax.random.normal(k, shape, f32) * scale

    return {
        'x_prompt': nrm(ks[0], (BATCH, SEQ, D), 1.0),
        'x_sample': nrm(ks[1], (DEC_BATCH, DEC_SEQ, D), 1.0),
        'p_prompt': nrm(ks[2], (DEPTH, BATCH, SEQ, PLE_DIM), 1.0),
        'p_sample': nrm(ks[3], (DEPTH, DEC_BATCH, DEC_SEQ, PLE_DIM), 1.0),
        'state_mlstm_C': nrm(ks[4], (N_MLSTM, DEC_BATCH, M_HEADS, M_DQK, M_DV), 0.5),
        'state_mlstm_n': nrm(ks[5], (N_MLSTM, DEC_BATCH, M_HEADS, M_DQK), 0.5),
        'state_mlstm_m': nrm(ks[6], (N_MLSTM, DEC_BATCH, M_HEADS), 1.0),
        'cache_fox_k': nrm(ks[7], (N_FOX, DEC_BATCH, PAST_LEN, F_HEADS, F_HD), 1.0),
        'cache_fox_v': nrm(ks[8], (N_FOX, DEC_BATCH, PAST_LEN, F_HEADS, F_HD), 1.0),
        'cache_fox_lf': jax.nn.log_sigmoid(3.0 + jax.random.normal(ks[9], (N_FOX, DEC_BATCH, PAST_LEN, F_HEADS), f32)),
        'norm_gains': 1.0 + nrm(ks[10], (DEPTH, 4, D), 0.1),
        'ffn1_in': nrm(ks[11], (DEPTH, D, 2 * D_FF), D ** -0.5),
        'ffn1_out': nrm(ks[12], (DEPTH, D_FF, D), D_FF ** -0.5),
        'ffn2_in': nrm(ks[13], (DEPTH, D, 2 * D_FF), D ** -0.5),
        'ffn2_out': nrm(ks[14], (DEPTH, D_FF, D), D_FF ** -0.5),
        'ple_gate': nrm(ks[15], (DEPTH, D, D), D ** -0.5),
        'ple_proj': nrm(ks[16], (DEPTH, PLE_DIM, D), PLE_DIM ** -0.5),
        'mlstm_w_in': nrm(ks[17], (N_MLSTM, D, M_IN), D ** -0.5),
        'mlstm_b_gates': jnp.stack([nrm(ks[18], (N_MLSTM, M_HEADS), 0.1),
                                    jax.random.uniform(ks[19], (N_MLSTM, M_HEADS), f32, 3.0, 6.0)], axis=1),
        'mlstm_g_h': 1.0 + nrm(ks[20], (N_MLSTM, M_V), 0.1),
        'mlstm_w_out': nrm(ks[21], (N_MLSTM, M_V, D), M_V ** -0.5),
        'fox_w_in': nrm(ks[22], (N_FOX, D, F_IN), D ** -0.5),
        'fox_b_f': jax.random.uniform(ks[23], (N_FOX, F_HEADS), f32, 2.0, 5.0),
        'fox_g_qk': 1.0 + nrm(ks[24], (N_FOX, 2, F_HD), 0.1),
        'fox_w_out': nrm(ks[25], (N_FOX, D, D), D ** -0.5),
    }


def reference(x_prompt, x_sample, p_prompt, p_sample, state_mlstm_C, state_mlstm_n, state_mlstm_m,
              cache_fox_k, cache_fox_v, cache_fox_lf, norm_gains, ffn1_in, ffn1_out, ffn2_in, ffn2_out,
              ple_gate, ple_proj, mlstm_w_in, mlstm_b_gates, mlstm_g_h, mlstm_w_out,
              fox_w_in, fox_b_f, fox_g_qk, fox_w_out):
    prm = {
        'norm_gains': norm_gains, 'ffn1_in': ffn1_in, 'ffn1_out': ffn1_out,
        'ffn2_in': ffn2_in, 'ffn2_out': ffn2_out, 'ple_gate': ple_gate, 'ple_proj': ple_proj,
        'mlstm_w_in': mlstm_w_in, 'mlstm_b_gates': mlstm_b_gates, 'mlstm_g_h': mlstm_g_h,
        'mlstm_w_out': mlstm_w_out, 'fox_w_in': fox_w_in, 'fox_b_f': fox_b_f,
        'fox_g_qk': fox_g_qk, 'fox_w_out': fox_w_out,
    }
    y_prompt, pC, pn, pm, pk, pv, plf = run_trunk(x_prompt, p_prompt, prm, None, None)
    y_sample, sC, sn, sm, sk, sv, slf = run_trunk(
        x_sample, p_sample, prm,
        (state_mlstm_C, state_mlstm_n, state_mlstm_m),
        (cache_fox_k, cache_fox_v, cache_fox_lf))
    return (y_prompt, y_sample, pC, pn, pm, pk, pv, plf, sC, sn, sm, sk, sv, slf)
```

```python
import numpy as np
from contextlib import ExitStack
import concourse.bass as bass
import concourse.mybir as mybir
from concourse.bass_utils import run_bass_kernel_spmd

F32 = mybir.dt.float32
BF16 = mybir.dt.bfloat16
AF = mybir.ActivationFunctionType
ALU = mybir.AluOpType
AX = mybir.AxisListType

D = 2048; KC = 16; DFF = 5632; FC = 44; DEPTH = 4
NPR = 1024; NSM = 32; TOK = NPR + NSM
NTL = [(0, 512), (512, 512), (1024, 32)]
TTL = [(i * 128, 128) for i in range(8)] + [(1024, 32)]
PAST = 4096
EPS = 1e-6
NEG = -30000.0
GROUPS = [[0, 1, 2, 3], [4, 5, 6, 7]]


class StopBuild(Exception):
    pass


class LazyIn:
    def __init__(s, nc, name, shape, dt, used):
        s.nc, s.name, s.shape, s.dt, s.used, s._ap = nc, name, shape, dt, used, None

    def ap(s):
        if s._ap is None:
            s._ap = s.nc.dram_tensor(s.name, s.shape, s.dt, kind="ExternalInput").ap()
            s.used.append(s.name)
        return s._ap

    def __getitem__(s, k):
        return s.ap()[k]

    def partition_broadcast(s, n):
        return s.ap().partition_broadcast(n)


class Bld:
    def __init__(s):
        s.nc = bass.Bass("TRN2", target_bir_lowering=False)
        s.es = ExitStack()
        nc = s.nc
        s.E = {'pe': nc.tensor, 'act': nc.scalar, 'dve': nc.vector, 'pool': nc.gpsimd, 'sp': nc.sync}
        s.semobj = {}
        s.cnt = {}
        for e in s.E:
            s.semobj[e] = s.es.enter_context(nc.semaphore("sem_" + e))
            s.cnt[e] = 0
        s.ND = 40
        s.dcount = [0] * s.ND
        s.dnext = 0
        for i in range(s.ND):
            s.semobj['d%d' % i] = s.es.enter_context(nc.semaphore("semd%d" % i))
        s.semobj['cc'] = s.es.enter_context(nc.semaphore("semcc"))
        s.cccount = 0
        s.waited = {e: {} for e in s.E}
        s.lastw = {}
        s.readers = {}
        s.psn = 0
        s.uid = 0
        s.held = set()

    def _wait(s, e, ev):
        name, val = ev
        if e == 'pe' and name == 'pe':
            return
        if s.waited[e].get(name, 0) >= val:
            return
        s.E[e].wait_ge(s.semobj[name], val)
        s.waited[e][name] = val

    def _deps(s, e, r, w):
        need = {}
        for k in r:
            for n, v in s.lastw.get(k, {}).items():
                need[n] = max(need.get(n, 0), v)
        for k in w:
            for n, v in s.lastw.get(k, {}).items():
                need[n] = max(need.get(n, 0), v)
            for n, v in s.readers.get(k, {}).items():
                need[n] = max(need.get(n, 0), v)
        for n, v in need.items():
            s._wait(e, (n, v))

    def _record(s, ev, r, w):
        for k in w:
            d = s.lastw.setdefault(k, {})
            d[ev[0]] = max(d.get(ev[0], 0), ev[1])
        for k in r:
            d = s.readers.setdefault(k, {})
            d[ev[0]] = max(d.get(ev[0], 0), ev[1])

    def op(s, e, fn, r=(), w=()):
        s._deps(e, r, w)
        ins = fn()
        s.cnt[e] += 1
        ins.then_inc(s.semobj[e], 1)
        s._record((e, s.cnt[e]), r, w)

    def dma(s, q, out, in_, r=(), w=()):
        slot = s.dnext
        s.dnext = (s.dnext + 1) % s.ND
        nm = 'd%d' % slot
        if s.dcount[slot] > 0:
            s._wait(q, (nm, 16 * s.dcount[slot]))
        s._deps(q, r, w)
        ins = s.E[q].dma_start(out=out, in_=in_)
        s.dcount[slot] += 1
        ins.then_inc(s.semobj[nm], 16)
        s._record((nm, 16 * s.dcount[slot]), r, w)

    def allgather(s, in_t, out_t, r=(), w=()):
        if NO_CC:
            n0 = in_t.ap().shape[0]
            s.dma('sp', out_t.ap()[0:n0, :], in_t.ap(), r=r, w=w)
            return
        s._deps('pool', r, w)
        ins = s.nc.gpsimd.collective_compute("AllGather", ALU.bypass, replica_groups=GROUPS,
                                             ins=[in_t.ap().opt()], outs=[out_t.ap().opt()])
        s.cccount += 1
        ins.then_inc(s.semobj['cc'])
        s._record(('cc', s.cccount), r, w)

    def barrier(s):
        evs = [(e, s.cnt[e]) for e in s.E if s.cnt[e] > 0]
        evs += [('d%d' % i, 16 * s.dcount[i]) for i in range(s.ND) if s.dcount[i] > 0]
        if s.cccount:
            evs.append(('cc', s.cccount))
        for e in s.E:
            for ev in evs:
                s._wait(e, ev)

    def sb(s, name, shape, dt=F32, es=None):
        return (es or s.es).enter_context(s.nc.sbuf_tensor(name, shape, dt))

    def ps(s, hold=False):
        for _ in range(8):
            i = s.psn
            s.psn = (s.psn + 1) % 8
            if i not in s.held:
                break
        if hold:
            s.held.add(i)
        return i

    def rel(s, *idx):
        for i in idx:
            s.held.discard(i)

    def key(s, base):
        s.uid += 1
        return "%s_u%d" % (base, s.uid)


def build(stop=None):
    b = Bld()
    nc = b.nc
    es = b.es
    b.used = []

    def din(name, shape, dt=F32):
        return LazyIn(nc, name, shape, dt, b.used)

    def stage(name):
        if stop == name:
            b.barrier()
            raise StopBuild()

    def dout(name, shape, dt=F32):
        return nc.dram_tensor(name, shape, dt, kind="ExternalOutput").ap()

    xin = din("xin", [TOK, D]); pin = din("pin", [DEPTH, TOK, 256])
    sC = din("sC", [2, 8, 128, 256]); sn = din("sn", [2, 8, 128]); sm = din("sm", [2, 8])
    ck = din("ck", [2, PAST, D]); cv = din("cv", [2, PAST, D]); clf = din("clf", [2, PAST, 16])
    ltin = din("lt", [1, 4])
    gains = din("norm_gains", [DEPTH * 4 * KC, 128])
    ffn_in = [din("ffn1_in", [DEPTH, D, 2 * DFF]), din("ffn2_in", [DEPTH, D, 2 * DFF])]
    ffn_out = [din("ffn1_out", [DEPTH, DFF, D]), din("ffn2_out", [DEPTH, DFF, D])]
    ple_gate = din("ple_gate", [DEPTH, D, D]); ple_proj = din("ple_proj", [DEPTH, 256, D])
    m_win = din("mlstm_w_in", [2, D, 6160]); m_bg = din("mlstm_b_gates", [2, 16])
    m_gh = din("mlstm_g_h", [2 * KC, 128]); m_wout = din("mlstm_w_out", [2, D, D])
    f_win = din("fox_w_in", [2, D, 8208]); f_bf = din("fox_b_f", [2, 16])
    f_gqk = din("fox_g_qk", [4, 128]); f_wout = din("fox_w_out", [2, D, D])

    y = dout("y", [TOK, D])
    Cst = dout("Cst", [2, 2, 8, 128, 256]); nst = dout("nst", [2, 2, 128, 8]); mst = dout("mst", [2, 2, 1, 8])
    kout = dout("kout", [2, TOK, D]); vout = dout("vout", [2, TOK, D]); lfout = dout("lfout", [2, TOK, 16])

    kv_tm = nc.dram_tensor("kv_tm", [TOK, 3072], BF16)
    mg_in = [nc.dram_tensor("mg_in%d" % i, [128, 514], F32) for i in range(4)] + [nc.dram_tensor("mg_in4", [128, 16], F32)]
    mg_out = [nc.dram_tensor("mg_out%d" % i, [4 * 128, 514], F32) for i in range(4)] + [nc.dram_tensor("mg_out4", [4 * 128, 16], F32)]
    kt_in = [nc.dram_tensor("kt_in%d" % i, [256, NPR], BF16) for i in range(8)]
    kt_out = [nc.dram_tensor("kt_out%d" % i, [4 * 256, NPR], BF16) for i in range(8)]
    v_in = [nc.dram_tensor("v_in%d" % i, [128, D], BF16) for i in range(8)]
    v_out = [nc.dram_tensor("v_out%d" % i, [4 * 128, D], BF16) for i in range(8)]
    ci_in = nc.dram_tensor("ci_in", [1152, 16], F32); ci_out = nc.dram_tensor("ci_out", [4 * 1152, 16], F32)

    xT = b.sb("xT", [128, KC, TOK], F32)
    A = b.sb("A", [128, KC, TOK], BF16)
    Bq = b.sb("Bq", [128, KC, TOK], BF16)
    NW2 = 3
    WB2 = [b.sb("WB2_%d" % i, [128, KC, 128], BF16) for i in range(NW2)]
    WB3 = [None, None]

    def alloc_wb3(les):
        for i in range(2):
            WB3[i] = b.sb(b.key("WB3"), [128, KC, 256], BF16, les)
    ident = b.sb("ident", [128, 128], F32); identb = b.sb("identb", [128, 128], BF16)
    onesf = b.sb("onesf", [128, 128], F32); onesb = b.sb("onesb", [128, 128], BF16)
    triuf = b.sb("triuf", [128, 128], F32)
    maskb = b.sb("maskb", [128, 4, 512], BF16)
    G = b.sb("G", [128, DEPTH * 4 * KC], F32)
    GH = b.sb("GH", [128, 2 * KC], F32)
    GQK = b.sb("GQK", [128, 4], F32)
    LT = b.sb("LT", [128, 4], F32)
    PS = [es.enter_context(nc.psum_tensor("ps%d" % i, [128, 512], F32)) for i in range(8)]
    w2n = [0]
    w3n = [0]
    tmpn = [0]
    NTMP = 3
    TF = [b.sb("TF%d" % i, [128, 512], F32) for i in range(NTMP)]
    TB = [b.sb("TB%d" % i, [128, 512], BF16) for i in range(NTMP)]
    tbn = [0]

    def tf():
        i = tmpn[0]; tmpn[0] = (i + 1) % NTMP
        return TF[i], 'TF%d' % i

    def tb():
        i = tbn[0]; tbn[0] = (i + 1) % NTMP
        return TB[i], 'TB%d' % i

    b.op('pool', lambda: nc.gpsimd.memset(onesf[:], 1.0), w=['onesf'])
    b.op('pool', lambda: nc.gpsimd.memset(onesb[:], 1.0), w=['onesb'])
    b.op('pool', lambda: nc.gpsimd.affine_select(out=ident[:], in_=onesf[:], pattern=[[1, 128]], compare_op=ALU.is_equal,
                                                 fill=0.0, base=0, channel_multiplier=-1), r=['onesf'], w=['ident'])
    b.op('pool', lambda: nc.gpsimd.affine_select(out=triuf[:], in_=onesf[:], pattern=[[1, 128]], compare_op=ALU.is_ge,
                                                 fill=0.0, base=0, channel_multiplier=-1), r=['onesf'], w=['triuf'])
    b.op('dve', lambda: nc.vector.tensor_copy(out=identb[:], in_=ident[:]), r=['ident'], w=['identb'])
    for i in range(4):
        b.op('pool', lambda i=i: nc.gpsimd.affine_select(out=maskb[:, i, :], in_=onesb[:, 0:1].to_broadcast([128, 512]),
                                                         pattern=[[1, 512]], compare_op=ALU.is_ge, fill=0.0,
                                                         base=-128 * i, channel_multiplier=-1), r=['onesb'], w=['maskb'])
    b.dma('sp', LT[:], ltin.partition_broadcast(128), w=['LT'])

    def load_cols(dst, dram2d, R, dkey):
        with ExitStack() as les:
            st = b.sb(b.key("lcst"), [128, 128], F32, les)
            done = 0
            while done < R:
                n = min(128, R - done)
                k = 'lcst'
                b.dma('sp', st[0:n, :], dram2d[done:done + n, :], w=[k])
                pi = b.ps()
                b.op('pe', lambda: nc.tensor.transpose(PS[pi][:, 0:n], st[0:n, :], ident[0:n, 0:n]), r=[k, 'ident'], w=['ps%d' % pi])
                b.op('dve', lambda: nc.vector.tensor_copy(out=dst[:, done:done + n], in_=PS[pi][:, 0:n]), r=['ps%d' % pi], w=[dkey])
                done += n
            b.barrier()

    load_cols(G, gains, DEPTH * 4 * KC, 'G')
    load_cols(GH, m_gh, 2 * KC, 'GH')
    load_cols(GQK, f_gqk, 4, 'GQK')

    def load_x():
        with ExitStack() as les:
            st = [b.sb("xst%d" % i, [128, D], F32, les) for i in range(2)]
            for ti, (t0, n) in enumerate(TTL):
                sk = 'xst%d' % (ti % 2)
                b.dma('sp', st[ti % 2][0:n, :], xin[t0:t0 + n, :], w=[sk])
                for g in range(4):
                    pi = b.ps()
                    for q in range(4):
                        kc = g * 4 + q
                        b.op('pe', lambda kc=kc, q=q: nc.tensor.transpose(PS[pi][:, q * 128:q * 128 + n], st[ti % 2][0:n, kc * 128:(kc + 1) * 128],
                                                                          ident[0:n, 0:n]), r=[sk, 'ident'], w=['ps%d' % pi])
                    b.op('dve' if g % 2 == 0 else 'act',
                         (lambda g=g: nc.vector.tensor_copy(out=xT[:, g * 4:g * 4 + 4, t0:t0 + n],
                                                            in_=PS[pi][:, :].rearrange("p (q t) -> p q t", q=4)[:, :, 0:n])) if g % 2 == 0 else
                         (lambda g=g: nc.scalar.copy(out=xT[:, g * 4:g * 4 + 4, t0:t0 + n],
                                                     in_=PS[pi][:, :].rearrange("p (q t) -> p q t", q=4)[:, :, 0:n])),
                         r=['ps%d' % pi], w=['xT'])
            b.barrier()

    load_x()

    def rstd_from_ps(pi, n, inv_d, dstf, dkey):
        b.op('act', lambda: nc.scalar.activation(out=dstf[:, 0:n], in_=PS[pi][:, 0:n], func=AF.Sqrt, bias=EPS, scale=inv_d),
             r=['ps%d' % pi], w=[dkey])
        b.op('dve', lambda: nc.vector.reciprocal(out=dstf[:, 0:n], in_=dstf[:, 0:n]), r=[dkey], w=[dkey])

    def rmsnorm(gidx, dst, dkey):
        for (t0, n) in NTL:
            pi = b.ps()
            for kc in range(KC):
                sq, sqk = tb()
                if kc % 2 == 0:
                    b.op('act', lambda kc=kc: nc.scalar.activation(out=sq[:, 0:n], in_=xT[:, kc, t0:t0 + n], func=AF.Square), r=['xT'], w=[sqk])
                else:
                    b.op('dve', lambda kc=kc: nc.vector.tensor_tensor(out=sq[:, 0:n], in0=xT[:, kc, t0:t0 + n], in1=xT[:, kc, t0:t0 + n], op=ALU.mult),
                         r=['xT'], w=[sqk])
                b.op('pe', lambda kc=kc: nc.tensor.matmul(PS[pi][:, 0:n], onesb[:, :], sq[:, 0:n], start=(kc == 0), stop=(kc == KC - 1)),
                     r=[sqk, 'onesb'], w=['ps%d' % pi])
            rs, rsk = tf()
            rstd_from_ps(pi, n, 1.0 / D, rs, rsk)
            for kc in range(KC):
                b.op('dve', lambda kc=kc: nc.vector.scalar_tensor_tensor(out=dst[:, kc, t0:t0 + n], in0=xT[:, kc, t0:t0 + n],
                                                                        scalar=G[:, gidx * KC + kc:gidx * KC + kc + 1], in1=rs[:, 0:n],
                                                                        op0=ALU.mult, op1=ALU.mult), r=['xT', rsk, 'G'], w=[dkey])

    def load_w2(W2d, kcs, c0, ncols=128):
        i = w2n[0]; w2n[0] = (i + 1) % NW2
        k0, nk = kcs
        src = W2d[k0 * 128:(k0 + nk) * 128, c0:c0 + ncols].rearrange("(kc p) n -> p kc n", p=128)
        b.dma('pool', WB2[i][:, 0:nk, 0:ncols], src, w=['WB2_%d' % i])
        return WB2[i], 'WB2_%d' % i

    def load_w3(W2d, c0, ncols):
        i = w3n[0]; w3n[0] = (i + 1) % 2
        src = W2d[:, c0:c0 + ncols].rearrange("(kc p) n -> p kc n", p=128)
        b.dma('pool', WB3[i][:, :, 0:ncols], src, w=['WB3_%d' % i])
        return WB3[i], 'WB3_%d' % i

    def mm_fm(pi, wt, wk, nk, src, skey, t0, n, ncols=128):
        for kc in range(nk):
            b.op('pe', lambda kc=kc: nc.tensor.matmul(PS[pi][0:ncols, 0:n], wt[:, kc, 0:ncols], src[:, kc, t0:t0 + n],
                                                      start=(kc == 0), stop=(kc == nk - 1)), r=[wk, skey], w=['ps%d' % pi])

    def gemm_fm(W2d, cols, src, skey, epi, nk=KC, k0=0):
        nxt = load_w2(W2d, (k0, nk), cols[0])
        for ci, c0 in enumerate(cols):
            wt, wk = nxt
            if ci + 1 < len(cols):
                nxt = load_w2(W2d, (k0, nk), cols[ci + 1])
            for (t0, n) in NTL:
                pi = b.ps()
                mm_fm(pi, wt, wk, nk, src, skey, t0, n)
                epi(ci, t0, n, pi)

    def gemm_tm(W2d, c0, ncols_total, src, skey, epi, blk=256):
        nb = (ncols_total + blk - 1) // blk
        nxt = load_w3(W2d, c0, min(blk, ncols_total))
        for cb in range(nb):
            wt, wk = nxt
            bw = min(blk, ncols_total - cb * blk)
            if cb + 1 < nb:
                nxt = load_w3(W2d, c0 + (cb + 1) * blk, min(blk, ncols_total - (cb + 1) * blk))
            for ti, (t0, n) in enumerate(TTL):
                pi = b.ps()
                for kc in range(KC):
                    b.op('pe', lambda kc=kc: nc.tensor.matmul(PS[pi][0:n, 0:bw], src[:, kc, t0:t0 + n], wt[:, kc, 0:bw],
                                                              start=(kc == 0), stop=(kc == KC - 1)), r=[wk, skey], w=['ps%d' % pi])
                epi(cb, ti, t0, n, pi, bw)

    def ffn(l, which, gidx):
        if SKIP_FFN:
            return
        Win = ffn_in[which][l]
        Wout = ffn_out[which][l]
        rmsnorm(gidx, A, 'A')
        NPASS = 4; CP = FC // NPASS
        for p in range(NPASS):
            nxt_g = load_w2(Win, (0, KC), (p * CP) * 128)
            for ci in range(CP):
                c = p * CP + ci
                wg, wgk = nxt_g
                wu, wuk = load_w2(Win, (0, KC), DFF + c * 128)
                if ci + 1 < CP:
                    nxt_g = load_w2(Win, (0, KC), (c + 1) * 128)
                pgs = []
                for (t0, n) in NTL:
                    pg = b.ps(True); mm_fm(pg, wg, wgk, KC, A, 'A', t0, n)
                    pgs.append(pg)
                for ni, (t0, n) in enumerate(NTL):
                    pg = pgs[ni]
                    pu = b.ps(True); mm_fm(pu, wu, wuk, KC, A, 'A', t0, n)
                    sg, sgk = tf()
                    b.op('act', lambda: nc.scalar.activation(out=sg[:, 0:n], in_=PS[pg][:, 0:n], func=AF.Silu), r=['ps%d' % pg], w=[sgk])
                    b.op('dve', lambda: nc.vector.tensor_tensor(out=Bq[:, ci, t0:t0 + n], in0=PS[pu][:, 0:n], in1=sg[:, 0:n], op=ALU.mult),
                         r=['ps%d' % pu, sgk], w=['Bq'])
                    b.rel(pg, pu)

            def epi(dc, t0, n, pi):
                b.op('dve', lambda: nc.vector.scalar_tensor_tensor(out=xT[:, dc, t0:t0 + n], in0=PS[pi][:, 0:n], scalar=0.5,
                                                                  in1=xT[:, dc, t0:t0 + n], op0=ALU.mult, op1=ALU.add),
                     r=['ps%d' % pi, 'xT'], w=['xT'])
            gemm_fm(Wout, [dc * 128 for dc in range(KC)], Bq, 'Bq', epi, nk=CP, k0=p * CP)

    def ple(l):
        rmsnorm(l * 4 + 3, A, 'A')
        with ExitStack() as les:
            pT = b.sb(b.key("pT"), [128, 2, TOK], BF16, les)
            st = [b.sb(b.key("pst"), [128, 256], F32, les) for _ in range(2)]
            for ti, (t0, n) in enumerate(TTL):
                sk = 'pst%d' % (ti % 2)
                b.dma('sp', st[ti % 2][0:n, :], pin[l, t0:t0 + n, :], w=[sk])
                pi = b.ps()
                for q in range(2):
                    b.op('pe', lambda q=q: nc.tensor.transpose(PS[pi][:, q * 128:q * 128 + n], st[ti % 2][0:n, q * 128:(q + 1) * 128], ident[0:n, 0:n]),
                         r=[sk, 'ident'], w=['ps%d' % pi])
                b.op('act', lambda: nc.scalar.copy(out=pT[:, :, t0:t0 + n], in_=PS[pi][:, 0:256].rearrange("p (q t) -> p q t", q=2)[:, :, 0:n]),
                     r=['ps%d' % pi], w=['pT'])
            Wg = ple_gate[l]; Wp = ple_proj[l]
            for c in range(KC):
                wg, wgk = load_w2(Wg, (0, KC), c * 128)
                wp, wpk = load_w2(Wp, (0, 2), c * 128)
                for (t0, n) in NTL:
                    pg = b.ps(); mm_fm(pg, wg, wgk, KC, A, 'A', t0, n)
                    pp = b.ps(); mm_fm(pp, wp, wpk, 2, pT, 'pT', t0, n)
                    sg, sgk = tf()
                    b.op('act', lambda: nc.scalar.activation(out=sg[:, 0:n], in_=PS[pg][:, 0:n], func=AF.Sigmoid), r=['ps%d' % pg], w=[sgk])
                    b.op('dve', lambda: nc.vector.tensor_tensor(out=sg[:, 0:n], in0=PS[pp][:, 0:n], in1=sg[:, 0:n], op=ALU.mult),
                         r=['ps%d' % pp, sgk], w=[sgk])
                    b.op('dve', lambda: nc.vector.tensor_tensor(out=xT[:, c, t0:t0 + n], in0=xT[:, c, t0:t0 + n], in1=sg[:, 0:n], op=ALU.add),
                         r=[sgk, 'xT'], w=['xT'])
            b.barrier()

    def gate_proj(W2d, c0, ng, dst, dkey):
        i = w3n[0]; w3n[0] = (i + 1) % 2
        src = W2d[:, c0:c0 + ng].rearrange("(kc p) n -> p kc n", p=128)
        b.dma('pool', WB3[i][:, :, 0:ng], src, w=['WB3_%d' % i])
        for ti, (t0, n) in enumerate(TTL):
            pi = b.ps()
            for kc in range(KC):
                b.op('pe', lambda kc=kc: nc.tensor.matmul(PS[pi][0:n, 0:ng], A[:, kc, t0:t0 + n], WB3[i][:, kc, 0:ng],
                                                          start=(kc == 0), stop=(kc == KC - 1)), r=['WB3_%d' % i, 'A'], w=['ps%d' % pi])
            b.op('dve', lambda ti=ti: nc.vector.tensor_copy(out=dst[0:n, ti, 0:ng], in_=PS[pi][0:n, 0:ng]), r=['ps%d' % pi], w=[dkey])

    def softplus_neg(dst, dkey, src, skey, bias_t, bkey, nh):
        for ti, (t0, n) in enumerate(TTL):
            b.op('dve', lambda ti=ti: nc.vector.tensor_tensor(out=dst[0:n, ti, :], in0=src[0:n, ti, :], in1=bias_t[0:n, :], op=ALU.add),
                 r=[skey, bkey], w=[dkey])
            b.op('act', lambda ti=ti: nc.scalar.activation(out=dst[0:n, ti, :], in_=dst[0:n, ti, :], func=AF.Exp, scale=-1.0), r=[dkey], w=[dkey])
            b.op('act', lambda ti=ti: nc.scalar.activation(out=dst[0:n, ti, :], in_=dst[0:n, ti, :], func=AF.Ln, bias=1.0), r=[dkey], w=[dkey])

    def cumsum_tiles(nlf, nkey, cum, ckey, pre, pkey, nh, tiles, reset_at=None):
        first = True
        for ti, (t0, n) in tiles:
            if first or ti == reset_at:
                b.op('dve', lambda ti=ti: nc.vector.memset(pre[:, ti, :], 0.0), w=[pkey])
                first = False
            pi = b.ps()
            b.op('pe', lambda ti=ti: nc.tensor.matmul(PS[pi][0:n, 0:nh], triuf[0:n, 0:n], nlf[0:n, ti, :], start=True, stop=True),
                 r=[nkey, 'triuf'], w=['ps%d' % pi])
            b.op('dve', lambda ti=ti: nc.vector.tensor_tensor(out=cum[0:n, ti, :], in0=PS[pi][0:n, 0:nh], in1=pre[0:n, ti, :], op=ALU.add),
                 r=['ps%d' % pi, pkey], w=[ckey])
            pj = b.ps()
            b.op('pe', lambda ti=ti: nc.tensor.matmul(PS[pj][:, 0:nh], onesf[0:n, :], nlf[0:n, ti, :], start=True, stop=True),
                 r=[nkey, 'onesf'], w=['ps%d' % pj])
            b.op('dve', lambda ti=ti: nc.vector.tensor_tensor(out=pre[:, ti + 1, :], in0=PS[pj][:, 0:nh], in1=pre[:, ti, :], op=ALU.add),
                 r=['ps%d' % pj, pkey], w=[pkey])

    def mlstm(l, j):
        W = m_win[j]
        rmsnorm(l * 4 + 1, A, 'A')
        with ExitStack() as les:
            Gt = b.sb(b.key("Gt"), [128, 9, 16], F32, les)
            nlf = b.sb(b.key("nlf"), [128, 9, 8], F32, les)
            nlfh = b.sb(b.key("nlfh"), [128, 9, 8], BF16, les)
            nlfl = b.sb(b.key("nlfl"), [128, 9, 8], BF16, les)
            cum = b.sb(b.key("cum"), [128, 9, 8], F32, les)
            pre = b.sb(b.key("pre"), [128, 10, 8], F32, les)
            alo = b.sb(b.key("alo"), [128, 9, 8], F32, les)
            bgt = b.sb(b.key("bgt"), [128, 16], F32, les)
            tmax = b.sb(b.key("tmax"), [8, 9], F32, les)
            mx4 = b.sb(b.key("mx4"), [8, 4], F32, les)
            MX = b.sb(b.key("MX"), [128, 4, 8], F32, les)
            wS = b.sb(b.key("wS"), [128, 9, 8], F32, les)
            WKq = b.sb(b.key("WKq"), [128, 9, 8], F32, les)
            Gsc = b.sb(b.key("Gsc"), [128, 24], F32, les)
            scal = b.sb(b.key("scal"), [128, 8, 8], F32, les)
            m0 = b.sb(b.key("m0"), [128, 8], F32, les)
            snT = b.sb(b.key("snT"), [8, 128], F32, les)
            b.dma('sp', bgt[:], m_bg[j:j + 1, :].partition_broadcast(128), w=['bgt'])
            pes = ExitStack()
            alloc_wb3(pes)
            kvst = [b.sb(b.key("kvst"), [128, 256], BF16, pes) for _ in range(4)]

            def epi_qk(ci, t0, n, pi):
                if ci < 8:
                    b.op('act', lambda: nc.scalar.copy(out=Bq[:, ci, t0:t0 + n], in_=PS[pi][:, 0:n]), r=['ps%d' % pi], w=['Bq'])
                else:
                    b.op('dve', lambda: nc.vector.tensor_scalar(out=Bq[:, ci, t0:t0 + n], in0=PS[pi][:, 0:n], scalar1=128.0 ** -0.5, scalar2=None,
                                                               op0=ALU.mult), r=['ps%d' % pi], w=['Bq'])
            gemm_fm(W, [c * 128 for c in range(16)], A, 'A', epi_qk)

            kvn = [0]

            def epi_kv(cb, ti, t0, n, pi, bw):
                i = kvn[0]; kvn[0] = (i + 1) % 4
                sk = 'kvst%d' % i
                if cb < 4:
                    b.op('dve', lambda: nc.vector.tensor_scalar(out=kvst[i][0:n, :], in0=PS[pi][0:n, 0:256], scalar1=128.0 ** -0.5, scalar2=None,
                                                               op0=ALU.mult), r=['ps%d' % pi], w=[sk])
                else:
                    b.op('act', lambda: nc.scalar.copy(out=kvst[i][0:n, :], in_=PS[pi][0:n, 0:256]), r=['ps%d' % pi], w=[sk])
                b.dma('sp', kv_tm.ap()[t0:t0 + n, cb * 256:(cb + 1) * 256], kvst[i][0:n, :], r=[sk], w=['kv_tm'])
            gemm_tm(W, 1024, 3072, A, 'A', epi_kv)

            gate_proj(W, 6144, 16, Gt, 'Gt')
            b.barrier()
            pes.close()
            Uloc = b.sb(b.key("Uloc"), [128, 8, 257], F32, les)
            C0 = b.sb(b.key("C0"), [128, 8, 257], F32, les)
            Kh = [b.sb(b.key("Kh"), [128, 9, 128], BF16, les) for _ in range(2)]
            Vh = [b.sb(b.key("Vh"), [128, 9, 257], BF16, les) for _ in range(2)]
            Kw = [b.sb(b.key("Kw"), [128, 128], BF16, les) for _ in range(2)]
            C0b = [b.sb(b.key("C0b"), [128, 257], BF16, les) for _ in range(2)]
            for i in range(2):
                b.op('pool', lambda i=i: nc.gpsimd.memset(Vh[i][:, :, 256:257], 1.0), w=['Vh%d' % i])
            softplus_neg(nlf, 'nlf', Gt[:, :, 8:16], 'Gt', bgt[:, 8:16], 'bgt', 8)
            b.op('dve', lambda: nc.vector.tensor_copy(out=nlfh[:], in_=nlf[:]), r=['nlf'], w=['nlfh'])
            b.op('dve', lambda: nc.vector.tensor_tensor(out=cum[:], in0=nlf[:], in1=nlfh[:], op=ALU.subtract), r=['nlf', 'nlfh'], w=['cum'])
            b.op('dve', lambda: nc.vector.tensor_copy(out=nlfl[:], in_=cum[:]), r=['cum'], w=['nlfl'])
            cumsum_tiles(nlf, 'nlf', cum, 'cum', pre, 'pre', 8, list(enumerate(TTL))[0:8])
            b.op('dve', lambda: nc.vector.memset(pre[:, 9, :], 0.0), w=['pre'])
            pi = b.ps()
            b.op('pe', lambda: nc.tensor.matmul(PS[pi][0:32, 0:8], triuf[0:32, 0:32], nlf[0:32, 8, :], start=True, stop=True),
                 r=['nlf', 'triuf'], w=['ps%d' % pi])
            b.op('dve', lambda: nc.vector.tensor_copy(out=cum[0:32, 8, :], in_=PS[pi][0:32, 0:8]), r=['ps%d' % pi], w=['cum'])
            for ti, (t0, n) in enumerate(TTL):
                b.op('dve', lambda ti=ti: nc.vector.tensor_tensor(out=alo[0:n, ti, :], in0=Gt[0:n, ti, 0:8], in1=bgt[0:n, 0:8], op=ALU.add),
                     r=['Gt', 'bgt'], w=['alo'])
                b.op('dve', lambda ti=ti: nc.vector.tensor_tensor(out=alo[0:n, ti, :], in0=alo[0:n, ti, :], in1=cum[0:n, ti, :], op=ALU.add),
                     r=['alo', 'cum'], w=['alo'])
                pi = b.ps()
                b.op('pe', lambda ti=ti: nc.tensor.transpose(PS[pi][0:8, 0:n], alo[0:n, ti, :], ident[0:n, 0:n]), r=['alo', 'ident'], w=['ps%d' % pi])
                b.op('dve', lambda ti=ti: nc.vector.tensor_reduce(out=tmax[:, ti:ti + 1], in_=PS[pi][0:8, 0:n], axis=AX.X, op=ALU.max), r=['ps%d' % pi], w=['tmax'])
            b.op('dve', lambda: nc.vector.tensor_reduce(out=mx4[:, 0:1], in_=tmax[:, 0:2], axis=AX.X, op=ALU.max), r=['tmax'], w=['mx4'])
            b.op('dve', lambda: nc.vector.tensor_reduce(out=mx4[:, 1:2], in_=tmax[:, 0:6], axis=AX.X, op=ALU.max), r=['tmax'], w=['mx4'])
            b.op('dve', lambda: nc.vector.tensor_reduce(out=mx4[:, 2:3], in_=tmax[:, 0:8], axis=AX.X, op=ALU.max), r=['tmax'], w=['mx4'])
            b.op('dve', lambda: nc.vector.tensor_copy(out=mx4[:, 3:4], in_=tmax[:, 8:9]), r=['tmax'], w=['mx4'])
            for q in range(4):
                pi = b.ps()
                b.op('pe', lambda q=q: nc.tensor.matmul(PS[pi][:, 0:8], mx4[:, q:q + 1].to_broadcast([8, 128]), ident[0:8, 0:8], start=True, stop=True),
                     r=['mx4', 'ident'], w=['ps%d' % pi])
                b.op('dve', lambda q=q: nc.vector.tensor_copy(out=MX[:, q, :], in_=PS[pi][:, 0:8]), r=['ps%d' % pi], w=['MX'])

            kvh = [0]

            def load_head(h, tiles):
                i = kvh[0]; kvh[0] = (i + 1) % 2
                if tiles == 'p':
                    b.dma('sp', Kh[i][:, 0:8, :], kv_tm.ap()[0:1024, h * 128:(h + 1) * 128].rearrange("(t p) c -> p t c", p=128), r=['kv_tm'], w=['Kh%d' % i])
                    b.dma('sp', Vh[i][:, 0:8, 0:256], kv_tm.ap()[0:1024, 1024 + h * 256:1024 + (h + 1) * 256].rearrange("(t p) c -> p t c", p=128),
                          r=['kv_tm'], w=['Vh%d' % i])
                else:
                    b.dma('sp', Kh[i][0:32, 8, :], kv_tm.ap()[1024:1056, h * 128:(h + 1) * 128], r=['kv_tm'], w=['Kh%d' % i])
                    b.dma('sp', Vh[i][0:32, 8, 0:256], kv_tm.ap()[1024:1056, 1024 + h * 256:1024 + (h + 1) * 256], r=['kv_tm'], w=['Vh%d' % i])
                return i

            kwn = [0]

            def summary(tiles, mxi, tl):
                for ti, (t0, n) in tl:
                    b.op('dve', lambda ti=ti: nc.vector.tensor_tensor(out=wS[0:n, ti, :], in0=alo[0:n, ti, :], in1=MX[0:n, mxi, :], op=ALU.subtract),
                         r=['alo', 'MX'], w=['wS'])
                    b.op('act', lambda ti=ti: nc.scalar.activation(out=wS[0:n, ti, :], in_=wS[0:n, ti, :], func=AF.Exp), r=['wS'], w=['wS'])
                for h in range(8):
                    i = load_head(h, tiles)
                    pi = b.ps()
                    for idx, (ti, (t0, n)) in enumerate(tl):
                        kk = kwn[0]; kwn[0] = (kk + 1) % 2
                        b.op('dve', lambda ti=ti, kk=kk: nc.vector.tensor_scalar(out=Kw[kk][0:n, :], in0=Kh[i][0:n, ti, :], scalar1=wS[0:n, ti, h:h + 1],
                                                                                 scalar2=None, op0=ALU.mult), r=['Kh%d' % i, 'wS'], w=['Kw%d' % kk])
                        b.op('pe', lambda ti=ti, kk=kk, idx=idx: nc.tensor.matmul(PS[pi][:, 0:257], Kw[kk][0:n, :], Vh[i][0:n, ti, :],
                                                                                  start=(idx == 0), stop=(idx == len(tl) - 1)),
                             r=['Kw%d' % kk, 'Vh%d' % i], w=['ps%d' % pi])
                    b.op('act', lambda h=h: nc.scalar.copy(out=Uloc[:, h, :], in_=PS[pi][:, 0:257]), r=['ps%d' % pi], w=['Uloc'])

            def bc8(t2d):
                return t2d.unsqueeze(2).to_broadcast([128, 8, 257])

            def finalize_state(mprev_key, which, BtN_ap, Me_ap):
                mxv = scal[:, 0, :]; e1 = scal[:, 1, :]; e2 = scal[:, 2, :]; mend = scal[:, 3, :]
                b.op('dve', lambda: nc.vector.tensor_tensor(out=mxv, in0=m0[:], in1=Me_ap, op=ALU.max), r=['m0', 'MX', 'Gsc'], w=['scal'])
                b.op('dve', lambda: nc.vector.tensor_tensor(out=e1, in0=m0[:], in1=mxv, op=ALU.subtract), r=['m0', 'scal'], w=['scal'])
                b.op('act', lambda: nc.scalar.activation(out=e1, in_=e1, func=AF.Exp), r=['scal'], w=['scal'])
                b.op('dve', lambda: nc.vector.tensor_tensor(out=e2, in0=Me_ap, in1=mxv, op=ALU.subtract), r=['MX', 'Gsc', 'scal'], w=['scal'])
                b.op('act', lambda: nc.scalar.activation(out=e2, in_=e2, func=AF.Exp), r=['scal'], w=['scal'])
                b.op('dve', lambda: nc.vector.tensor_tensor(out=mend, in0=mxv, in1=BtN_ap, op=ALU.subtract), r=['scal', 'pre', 'Gsc'], w=['scal'])
                b.op('dve', lambda: nc.vector.tensor_tensor(out=Uloc[:], in0=Uloc[:], in1=bc8(e2), op=ALU.mult), r=['Uloc', 'scal'], w=['Uloc'])
                b.op('pool', lambda: nc.gpsimd.tensor_tensor(out=C0[:], in0=C0[:], in1=bc8(e1), op=ALU.mult), r=['C0', 'scal'], w=['C0'])
                b.op('dve', lambda: nc.vector.tensor_tensor(out=Uloc[:], in0=Uloc[:], in1=C0[:], op=ALU.add), r=['Uloc', 'C0'], w=['Uloc'])
                b.dma('sp', Cst[j, which].rearrange("h d v -> d h v"), Uloc[:, :, 0:256], r=['Uloc'], w=['Cst'])
                b.op('dve', lambda: nc.vector.tensor_copy(out=scal[:, 7, :], in_=Uloc[:, :, 256]), r=['Uloc'], w=['scal'])
                b.dma('sp', nst[j, which], scal[:, 7, :], r=['scal'], w=['nst'])
                b.dma('sp', mst[j, which], mend[0:1, :], r=['scal'], w=['mst'])

            def attn_block(q0, nq, ktiles, mref_ap, pre_idx, masks_from):
                for ti, (t0, n) in ktiles:
                    b.op('dve', lambda ti=ti: nc.vector.tensor_tensor(out=WKq[0:n, ti, :], in0=alo[0:n, ti, :], in1=mref_ap[0:n, :], op=ALU.subtract),
                         r=['alo', 'scal'], w=['WKq'])
                    b.op('act', lambda ti=ti: nc.scalar.activation(out=WKq[0:n, ti, :], in_=WKq[0:n, ti, :], func=AF.Exp), r=['WKq'], w=['WKq'])
                e0 = scal[:, 5, :]; flb = scal[:, 6, :]
                b.op('dve', lambda: nc.vector.tensor_tensor(out=e0, in0=m0[:], in1=mref_ap, op=ALU.subtract), r=['m0', 'scal'], w=['scal'])
                b.op('act', lambda: nc.scalar.activation(out=e0, in_=e0, func=AF.Exp), r=['scal'], w=['scal'])
                b.op('dve', lambda: nc.vector.tensor_tensor(out=flb, in0=pre[:, pre_idx, :], in1=mref_ap, op=ALU.subtract), r=['pre', 'scal'], w=['scal'])
                for h in range(8):
                    i = load_head(h, 'p' if nq == 512 else 's')
                    cb = h % 2
                    b.op('dve', lambda h=h: nc.vector.tensor_scalar(out=C0b[cb][:], in0=C0[:, h, :], scalar1=e0[:, h:h + 1], scalar2=None, op0=ALU.mult),
                         r=['C0', 'scal'], w=['C0b%d' % cb])
                    pN0 = b.ps(True); pN1 = b.ps(True); pD = b.ps(True)
                    qT = Bq[:, h, q0:q0 + nq]
                    b.op('pe', lambda: nc.tensor.matmul(PS[pN0][:, 0:nq], C0b[cb][:, 0:128], qT, start=True, stop=False), r=['C0b%d' % cb, 'Bq'], w=['ps%d' % pN0])
                    b.op('pe', lambda: nc.tensor.matmul(PS[pN1][:, 0:nq], C0b[cb][:, 128:256], qT, start=True, stop=False), r=['C0b%d' % cb, 'Bq'], w=['ps%d' % pN1])
                    b.op('pe', lambda: nc.tensor.matmul(PS[pD][:, 0:nq], C0b[cb][:, 256:257].to_broadcast([128, 128]), qT, start=True, stop=False),
                         r=['C0b%d' % cb, 'Bq'], w=['ps%d' % pD])
                    for idx, (ti, (t0, n)) in enumerate(ktiles):
                        last = idx == len(ktiles) - 1
                        pS = b.ps()
                        b.op('pe', lambda: nc.tensor.matmul(PS[pS][0:n, 0:nq], Bq[:, 8 + h, t0:t0 + n], qT, start=True, stop=True), r=['Bq'], w=['ps%d' % pS])
                        wt, wtk = tb()
                        b.op('act', lambda ti=ti, h=h: nc.scalar.mul(out=wt[0:n, 0:nq], in_=PS[pS][0:n, 0:nq], mul=WKq[0:n, ti, h:h + 1]),
                             r=['ps%d' % pS, 'WKq'], w=[wtk])
                        mi = ti - masks_from
                        if mi >= 0:
                            b.op('dve', lambda mi=mi: nc.vector.tensor_tensor(out=wt[0:n, 0:nq], in0=wt[0:n, 0:nq], in1=maskb[0:n, mi, 0:nq], op=ALU.mult),
                                 r=[wtk, 'maskb'], w=[wtk])
                        b.op('pe', lambda ti=ti: nc.tensor.matmul(PS[pN0][:, 0:nq], Vh[i][0:n, ti, 0:128], wt[0:n, 0:nq], start=False, stop=last),
                             r=[wtk, 'Vh%d' % i], w=['ps%d' % pN0])
                        b.op('pe', lambda ti=ti: nc.tensor.matmul(PS[pN1][:, 0:nq], Vh[i][0:n, ti, 128:256], wt[0:n, 0:nq], start=False, stop=last),
                             r=[wtk, 'Vh%d' % i], w=['ps%d' % pN1])
                        b.op('pe', lambda: nc.tensor.matmul(PS[pD][:, 0:nq], onesb[0:n, :], wt[0:n, 0:nq], start=False, stop=last),
                             r=[wtk, 'onesb'], w=['ps%d' % pD])
                    pF = b.ps(True)
                    srcs = [(ti, tn) for (ti, tn) in ktiles if ti >= masks_from]
                    for idx, (ti, (t0, n)) in enumerate(srcs):
                        mi = ti - masks_from
                        b.op('pe', lambda ti=ti, mi=mi, idx=idx: nc.tensor.matmul(PS[pF][:, 0:nq], nlfh[0:n, ti, h:h + 1].to_broadcast([n, 128]), maskb[0:n, mi, 0:nq],
                                                                                  start=(idx == 0), stop=False), r=['nlfh', 'maskb'], w=['ps%d' % pF])
                        b.op('pe', lambda ti=ti, mi=mi, idx=idx: nc.tensor.matmul(PS[pF][:, 0:nq], nlfl[0:n, ti, h:h + 1].to_broadcast([n, 128]), maskb[0:n, mi, 0:nq],
                                                                                  start=False, stop=(idx == len(srcs) - 1)), r=['nlfl', 'maskb'], w=['ps%d' % pF])
                    fl, flk = tf()
                    b.op('act', lambda h=h: nc.scalar.activation(out=fl[:, 0:nq], in_=PS[pF][:, 0:nq], func=AF.Exp, bias=flb[:, h:h + 1], scale=1.0),
                         r=['ps%d' % pF, 'scal'], w=[flk])
                    dn, dnk = tf()
                    b.op('act', lambda: nc.scalar.copy(out=dn[:, 0:nq], in_=PS[pD][:, 0:nq]), r=['ps%d' % pD], w=[dnk])
                    b.op('dve', lambda: nc.vector.scalar_tensor_tensor(out=dn[:, 0:nq], in0=dn[:, 0:nq], scalar=-1.0, in1=dn[:, 0:nq], op0=ALU.mult, op1=ALU.max),
                         r=[dnk], w=[dnk])
                    b.op('dve', lambda: nc.vector.tensor_tensor(out=dn[:, 0:nq], in0=dn[:, 0:nq], in1=fl[:, 0:nq], op=ALU.max), r=[dnk, flk], w=[dnk])
                    b.op('dve', lambda: nc.vector.reciprocal(out=dn[:, 0:nq], in_=dn[:, 0:nq]), r=[dnk], w=[dnk])
                    h0, h0k = tf(); h1, h1k = tf()
                    b.op('dve', lambda: nc.vector.tensor_tensor(out=h0[:, 0:nq], in0=PS[pN0][:, 0:nq], in1=dn[:, 0:nq], op=ALU.mult), r=['ps%d' % pN0, dnk], w=[h0k])
                    b.op('dve', lambda: nc.vector.tensor_tensor(out=h1[:, 0:nq], in0=PS[pN1][:, 0:nq], in1=dn[:, 0:nq], op=ALU.mult), r=['ps%d' % pN1, dnk], w=[h1k])
                    s0, s0k = tb(); s1, s1k = tb()
                    b.op('act', lambda: nc.scalar.activation(out=s0[:, 0:nq], in_=h0[:, 0:nq], func=AF.Square), r=[h0k], w=[s0k])
                    b.op('act', lambda: nc.scalar.activation(out=s1[:, 0:nq], in_=h1[:, 0:nq], func=AF.Square), r=[h1k], w=[s1k])
                    pR = b.ps()
                    b.op('pe', lambda: nc.tensor.matmul(PS[pR][:, 0:nq], onesb[:, :], s0[:, 0:nq], start=True, stop=False), r=[s0k, 'onesb'], w=['ps%d' % pR])
                    b.op('pe', lambda: nc.tensor.matmul(PS[pR][:, 0:nq], onesb[:, :], s1[:, 0:nq], start=False, stop=True), r=[s1k, 'onesb'], w=['ps%d' % pR])
                    rs, rsk = tf()
                    rstd_from_ps(pR, nq, 1.0 / 256, rs, rsk)
                    for half, (hh, hk) in enumerate([(h0, h0k), (h1, h1k)]):
                        c = 2 * h + half
                        b.op('dve', lambda c=c, hh=hh: nc.vector.scalar_tensor_tensor(out=A[:, c, q0:q0 + nq], in0=hh[:, 0:nq],
                                                                                      scalar=GH[:, j * KC + c:j * KC + c + 1], in1=rs[:, 0:nq],
                                                                                      op0=ALU.mult, op1=ALU.mult), r=[hk, rsk, 'GH'], w=['A'])
                    b.rel(pN0, pN1, pD, pF)

            summary('p', 2, list(enumerate(TTL))[0:8])
            for c4 in range(4):
                b.dma('sp', mg_in[c4].ap().rearrange("p (h v) -> p h v", h=2), Uloc[:, 2 * c4:2 * c4 + 2, :], r=['Uloc'], w=['mg_in'])
            b.dma('sp', mg_in[4].ap()[:, 0:8], pre[:, 8, :], r=['pre'], w=['mg_in'])
            b.dma('sp', mg_in[4].ap()[:, 8:16], MX[:, 2, :], r=['MX'], w=['mg_in'])
            for c4 in range(5):
                b.allgather(mg_in[c4], mg_out[c4], r=['mg_in'], w=['mg_out'])

            summary('s', 3, [(8, TTL[8])])
            b.dma('sp', C0[:, :, 0:256], sC[j].rearrange("h d v -> d h v"), w=['C0'])
            b.dma('sp', snT[:], sn[j], w=['snT'])
            pi = b.ps()
            b.op('pe', lambda: nc.tensor.transpose(PS[pi][:, 0:8], snT[:, :], ident[0:8, 0:8]), r=['snT', 'ident'], w=['ps%d' % pi])
            b.op('dve', lambda: nc.vector.tensor_copy(out=C0[:, :, 256], in_=PS[pi][:, 0:8]), r=['ps%d' % pi], w=['C0'])
            b.dma('sp', m0[:], sm[j:j + 1, :].partition_broadcast(128), w=['m0'])
            mrefS = scal[:, 4, :]
            b.op('dve', lambda: nc.vector.tensor_tensor(out=mrefS, in0=m0[:], in1=MX[:, 3, :], op=ALU.max), r=['m0', 'MX'], w=['scal'])
            attn_block(1024, 32, [(8, TTL[8])], mrefS, 9, 8)
            pi = b.ps()
            b.op('pe', lambda: nc.tensor.matmul(PS[pi][:, 0:8], onesf[0:32, :], nlf[0:32, 8, :], start=True, stop=True), r=['nlf', 'onesf'], w=['ps%d' % pi])
            b.op('dve', lambda: nc.vector.tensor_copy(out=scal[:, 7, :], in_=PS[pi][:, 0:8]), r=['ps%d' % pi], w=['scal'])
            finalize_state('m0', 1, scal[:, 7, :], MX[:, 3, :])

            b.op('dve', lambda: nc.vector.memset(C0[:], 0.0), w=['C0'])
            b.op('dve', lambda: nc.vector.memset(m0[:], 0.0), w=['m0'])
            for i in range(3):
                for c4 in range(4):
                    b.dma('sp', Uloc[:, 2 * c4:2 * c4 + 2, :], mg_out[c4].ap()[i * 128:(i + 1) * 128, :].rearrange("p (h v) -> p h v", h=2), r=['mg_out'], w=['Uloc'])
                b.dma('sp', Gsc[:, 0:16], mg_out[4].ap()[i * 128:(i + 1) * 128, :], r=['mg_out'], w=['Gsc'])
                BtN = Gsc[:, 0:8]; Me = Gsc[:, 8:16]
                mxv = scal[:, 0, :]; e1 = scal[:, 1, :]; e2 = scal[:, 2, :]; mc = scal[:, 3, :]
                lti = LT[:, i:i + 1]
                b.op('dve', lambda: nc.vector.tensor_tensor(out=mxv, in0=m0[:], in1=Me, op=ALU.max), r=['m0', 'Gsc'], w=['scal'])
                b.op('dve', lambda: nc.vector.tensor_tensor(out=e1, in0=m0[:], in1=mxv, op=ALU.subtract), r=['m0', 'scal'], w=['scal'])
                b.op('act', lambda: nc.scalar.activation(out=e1, in_=e1, func=AF.Exp), r=['scal'], w=['scal'])
                b.op('dve', lambda: nc.vector.tensor_tensor(out=e2, in0=Me, in1=mxv, op=ALU.subtract), r=['Gsc', 'scal'], w=['scal'])
                b.op('act', lambda: nc.scalar.activation(out=e2, in_=e2, func=AF.Exp), r=['scal'], w=['scal'])
                b.op('dve', lambda: nc.vector.tensor_scalar(out=e1, in0=e1, scalar1=-1.0, scalar2=lti, op0=ALU.add, op1=ALU.mult), r=['scal', 'LT'], w=['scal'])
                b.op('dve', lambda: nc.vector.tensor_scalar(out=e1, in0=e1, scalar1=1.0, scalar2=None, op0=ALU.add), r=['scal'], w=['scal'])
                b.op('dve', lambda: nc.vector.tensor_scalar(out=e2, in0=e2, scalar1=lti, scalar2=None, op0=ALU.mult), r=['scal', 'LT'], w=['scal'])
                b.op('dve', lambda: nc.vector.tensor_tensor(out=mc, in0=mxv, in1=BtN, op=ALU.subtract), r=['scal', 'Gsc'], w=['scal'])
                b.op('dve', lambda: nc.vector.tensor_tensor(out=mc, in0=mc, in1=m0[:], op=ALU.subtract), r=['scal', 'm0'], w=['scal'])
                b.op('dve', lambda: nc.vector.scalar_tensor_tensor(out=m0[:], in0=mc, scalar=lti, in1=m0[:], op0=ALU.mult, op1=ALU.add),
                     r=['scal', 'm0', 'LT'], w=['m0'])
                b.op('dve', lambda: nc.vector.tensor_tensor(out=Uloc[:], in0=Uloc[:], in1=bc8(e2), op=ALU.mult), r=['Uloc', 'scal'], w=['Uloc'])
                b.op('pool', lambda: nc.gpsimd.tensor_tensor(out=C0[:], in0=C0[:], in1=bc8(e1), op=ALU.mult), r=['C0', 'scal'], w=['C0'])
                b.op('dve', lambda: nc.vector.tensor_tensor(out=C0[:], in0=C0[:], in1=Uloc[:], op=ALU.add), r=['Uloc', 'C0'], w=['C0'])
            ptl = list(enumerate(TTL))
            mref0 = scal[:, 4, :]
            b.op('dve', lambda: nc.vector.tensor_tensor(out=mref0, in0=m0[:], in1=MX[:, 0, :], op=ALU.max), r=['m0', 'MX'], w=['scal'])
            attn_block(0, 512, ptl[0:4], mref0, 0, 0)
            b.op('dve', lambda: nc.vector.tensor_tensor(out=mref0, in0=m0[:], in1=MX[:, 1, :], op=ALU.max), r=['m0', 'MX'], w=['scal'])
            attn_block(512, 512, ptl[0:8], mref0, 4, 4)
            for c4 in range(4):
                b.dma('sp', Uloc[:, 2 * c4:2 * c4 + 2, :], mg_in[c4].ap().rearrange("p (h v) -> p h v", h=2), r=['mg_in'], w=['Uloc'])
            finalize_state('m0', 0, pre[:, 8, :], MX[:, 2, :])

            rmsnorm(l * 4 + 1, Bq, 'Bq')

            def epi_o(ci, t0, n, pi):
                sg, sgk = tb()
                b.op('act', lambda: nc.scalar.activation(out=sg[:, 0:n], in_=PS[pi][:, 0:n], func=AF.Sigmoid), r=['ps%d' % pi], w=[sgk])
                b.op('dve', lambda: nc.vector.tensor_tensor(out=A[:, ci, t0:t0 + n], in0=A[:, ci, t0:t0 + n], in1=sg[:, 0:n], op=ALU.mult),
                     r=[sgk, 'A'], w=['A'])
            gemm_fm(W, [4096 + c * 128 for c in range(16)], Bq, 'Bq', epi_o)

            def epi_out(ci, t0, n, pi):
                b.op('dve', lambda: nc.vector.tensor_tensor(out=xT[:, ci, t0:t0 + n], in0=PS[pi][:, 0:n], in1=xT[:, ci, t0:t0 + n], op=ALU.add),
                     r=['ps%d' % pi, 'xT'], w=['xT'])
            gemm_fm(m_wout[j], [c * 128 for c in range(16)], A, 'A', epi_out)
            b.barrier()

    def fox(l, j):
        W = f_win[j]
        rmsnorm(l * 4 + 1, A, 'A')
        with ExitStack() as les:
            Gt = b.sb(b.key("fGt"), [128, 9, 16], F32, les)
            nlf = b.sb(b.key("fnlf"), [128, 9, 16], F32, les)
            cum = b.sb(b.key("fcum"), [128, 9, 16], F32, les)
            pre = b.sb(b.key("fpre"), [128, 10, 16], F32, les)
            bft = b.sb(b.key("bft"), [128, 16], F32, les)
            gsc = b.sb(b.key("gsc"), [128, 2], F32, les)
            KTs = b.sb(b.key("KTs"), [128, 16, 32], BF16, les)
            Vs = b.sb(b.key("Vs"), [32, D], BF16, les)
            lfo = b.sb(b.key("lfo"), [128, 9, 16], F32, les)
            pes = ExitStack()
            les.callback(pes.close)
            alloc_wb3(pes)
            kf = [b.sb(b.key("kf"), [128, 512], F32, pes) for _ in range(2)]
            kst = [b.sb(b.key("kst"), [128, 4, 128], F32, pes) for _ in range(2)]
            ktb = [b.sb(b.key("ktb"), [128, 512], BF16, pes) for _ in range(2)]
            vst = [b.sb(b.key("vst"), [128, 256], F32, pes) for _ in range(2)]
            vsb = [b.sb(b.key("vsb"), [128, 256], BF16, pes) for _ in range(2)]
            b.dma('sp', bft[:], f_bf[j:j + 1, :].partition_broadcast(128), w=['bft'])
            b.op('dve', lambda: nc.vector.tensor_scalar(out=gsc[:, 0:1], in0=GQK[:, 2 * j:2 * j + 1], scalar1=128.0 ** -0.5, scalar2=None, op0=ALU.mult),
                 r=['GQK'], w=['gsc'])
            b.op('dve', lambda: nc.vector.tensor_copy(out=gsc[:, 1:2], in_=GQK[:, 2 * j + 1:2 * j + 2]), r=['GQK'], w=['gsc'])

            kn = [0]

            def qknorm(pi, n, gcol):
                sq, sqk = tb()
                b.op('act', lambda: nc.scalar.activation(out=sq[:, 0:n], in_=PS[pi][:, 0:n], func=AF.Square), r=['ps%d' % pi], w=[sqk])
                pr = b.ps()
                b.op('pe', lambda: nc.tensor.matmul(PS[pr][:, 0:n], onesb[:, :], sq[:, 0:n], start=True, stop=True), r=[sqk, 'onesb'], w=['ps%d' % pr])
                rs, rsk = tf()
                rstd_from_ps(pr, n, 1.0 / 128, rs, rsk)
                return rs, rsk

            def epi_q(ci, t0, n, pi):
                rs, rsk = qknorm(pi, n, 0)
                b.op('dve', lambda: nc.vector.scalar_tensor_tensor(out=Bq[:, ci, t0:t0 + n], in0=PS[pi][:, 0:n], scalar=gsc[:, 0:1], in1=rs[:, 0:n],
                                                                  op0=ALU.mult, op1=ALU.mult), r=['ps%d' % pi, rsk, 'gsc'], w=['Bq'])
            gemm_fm(W, [c * 128 for c in range(16)], A, 'A', epi_q)
            stage('fox_q')

            def epi_k(ci, t0, n, pi):
                rs, rsk = qknorm(pi, n, 1)
                i = kn[0]; kn[0] = (i + 1) % 2
                b.op('dve', lambda: nc.vector.scalar_tensor_tensor(out=kf[i][:, 0:n], in0=PS[pi][:, 0:n], scalar=gsc[:, 1:2], in1=rs[:, 0:n],
                                                                  op0=ALU.mult, op1=ALU.mult), r=['ps%d' % pi, rsk, 'gsc'], w=['kf%d' % i])
                if n == 32:
                    b.op('act', lambda: nc.scalar.copy(out=KTs[:, ci, :], in_=kf[i][:, 0:32]), r=['kf%d' % i], w=['KTs'])
                else:
                    b.op('act', lambda: nc.scalar.copy(out=ktb[i][:, 0:n], in_=kf[i][:, 0:n]), r=['kf%d' % i], w=['ktb%d' % i])
                    b.dma('sp', kt_in[ci // 2].ap()[(ci % 2) * 128:(ci % 2 + 1) * 128, t0:t0 + n], ktb[i][:, 0:n], r=['ktb%d' % i], w=['kt_in'])
                nsub = (n + 127) // 128
                pt = b.ps()
                for q in range(nsub):
                    rows = min(128, n - q * 128)
                    b.op('pe', lambda q=q, rows=rows: nc.tensor.transpose(PS[pt][0:rows, q * 128:(q + 1) * 128], kf[i][:, q * 128:q * 128 + rows], ident[:, :]),
                         r=['kf%d' % i, 'ident'], w=['ps%d' % pt])
                if n == 32:
                    b.op('act', lambda: nc.scalar.copy(out=kst[i][0:32, 0, :], in_=PS[pt][0:32, 0:128]), r=['ps%d' % pt], w=['kst%d' % i])
                    b.dma('sp', kout[j, t0:t0 + 32, ci * 128:(ci + 1) * 128], kst[i][0:32, 0, :], r=['kst%d' % i], w=['kout'])
                else:
                    b.op('act', lambda: nc.scalar.copy(out=kst[i][:, :, :], in_=PS[pt][:, :].rearrange("p (q d) -> p q d", q=4)), r=['ps%d' % pt], w=['kst%d' % i])
                    b.dma('sp', kout[j, t0:t0 + n, ci * 128:(ci + 1) * 128].rearrange("(q p) d -> p q d", p=128), kst[i][:, :, :], r=['kst%d' % i], w=['kout'])
            gemm_fm(W, [D + c * 128 for c in range(16)], A, 'A', epi_k)
            stage('fox_k')

            vn = [0]

            def epi_v(cb, ti, t0, n, pi, bw):
                i = vn[0]; vn[0] = (i + 1) % 2
                b.op('act', lambda: nc.scalar.copy(out=vst[i][0:n, :], in_=PS[pi][0:n, 0:256]), r=['ps%d' % pi], w=['vst%d' % i])
                b.dma('sp', vout[j, t0:t0 + n, cb * 256:(cb + 1) * 256], vst[i][0:n, :], r=['vst%d' % i], w=['vout'])
                if FOXV == 1:
                    return
                if ti == 8:
                    if FOXV == 2:
                        return
                    b.op('dve', lambda: nc.vector.tensor_copy(out=Vs[0:32, cb * 256:(cb + 1) * 256], in_=vst[i][0:32, :]), r=['vst%d' % i], w=['Vs'])
                else:
                    b.op('dve', lambda: nc.vector.tensor_copy(out=vsb[i][0:n, :], in_=vst[i][0:n, :]), r=['vst%d' % i], w=['vsb%d' % i])
                    b.dma('sp', v_in[ti].ap()[0:n, cb * 256:(cb + 1) * 256], vsb[i][0:n, :], r=['vsb%d' % i], w=['v_in'])
            gemm_tm(W, 2 * D, D, A, 'A', epi_v)

            stage('fox_qkv')
            gate_proj(W, 4 * D, 16, Gt, 'fGt')
            b.barrier()
            pes.close()
            stage('fox_proj')
            KT = [b.sb(b.key("KT"), [128, 4096], BF16, les) for _ in range(2)]
            VV = [b.sb(b.key("VV"), [128, 32, 128], BF16, les) for _ in range(2)]
            ses = ExitStack()
            les.callback(ses.close)
            lfp = b.sb(b.key("lfp"), [128, 32, 16], F32, ses)
            cup = b.sb(b.key("cup"), [128, 32, 16], F32, ses)
            prp = b.sb(b.key("prp"), [128, 33, 16], F32, ses)
            softplus_neg(nlf, 'fnlf', Gt, 'fGt', bft, 'bft', 16)
            for ti, (t0, n) in enumerate(TTL):
                b.op('dve', lambda ti=ti: nc.vector.tensor_scalar(out=lfo[0:n, ti, :], in0=nlf[0:n, ti, :], scalar1=-1.0, scalar2=None, op0=ALU.mult),
                     r=['fnlf'], w=['lfo'])
            b.dma('sp', lfout[j, 0:1024, :].rearrange("(t p) h -> p t h", p=128), lfo[:, 0:8, :], r=['lfo'], w=['lfout'])
            b.dma('sp', lfout[j, 1024:1056, :], lfo[0:32, 8, :], r=['lfo'], w=['lfout'])
            cumsum_tiles(nlf, 'fnlf', cum, 'fcum', pre, 'fpre', 16, list(enumerate(TTL))[0:8])
            pi = b.ps()
            b.op('pe', lambda: nc.tensor.matmul(PS[pi][0:32, 0:16], triuf[0:32, 0:32], nlf[0:32, 8, :], start=True, stop=True),
                 r=['fnlf', 'triuf'], w=['ps%d' % pi])
            b.op('dve', lambda: nc.vector.tensor_copy(out=cum[0:32, 8, :], in_=PS[pi][0:32, 0:16]), r=['ps%d' % pi], w=['fcum'])
            b.dma('sp', ci_in.ap()[0:1024, :].rearrange("(t p) h -> p t h", p=128), cum[:, 0:8, :], r=['fcum'], w=['ci_in'])
            b.dma('sp', ci_in.ap()[1024:1152, :], pre[:, 8, :], r=['fpre'], w=['ci_in'])
            b.allgather(ci_in, ci_out, r=['ci_in'], w=['ci_out'])
            for c8 in range(8):
                b.allgather(kt_in[c8], kt_out[c8], r=['kt_in'], w=['kt_out'])
                b.allgather(v_in[c8], v_out[c8], r=['v_in'], w=['v_out'])

            stage('fox_gates')
            b.dma('sp', lfp[:], clf[j].rearrange("(t p) h -> p t h", p=128), w=['lfp'])
            cumsum_tiles(lfp, 'lfp', cup, 'cup', prp, 'prp', 16, [(t, (t * 128, 128)) for t in range(32)])
            for t in range(32):
                b.op('dve', lambda t=t: nc.vector.tensor_tensor(out=cup[:, t, :], in0=prp[:, 32, :], in1=cup[:, t, :], op=ALU.subtract),
                     r=['prp', 'cup'], w=['cup'])
            kvn = [0]
            for h in range(16):
                i = kvn[0]; kvn[0] = (i + 1) % 2
                b.dma('pool', VV[i][:], ck[j][:, h * 128:(h + 1) * 128].rearrange("(t p) d -> p t d", p=128), w=['VV%d' % i])
                for g in range(8):
                    pt = b.ps()
                    ptb = PS[pt][:, :].bitcast(BF16)
                    for q in range(4):
                        t = g * 4 + q
                        b.op('pe', lambda t=t, q=q: nc.tensor.transpose(ptb[:, q * 128:(q + 1) * 128], VV[i][:, t, :], identb[:, :]),
                             r=['VV%d' % i, 'identb'], w=['ps%d' % pt])
                    b.op('act' if g % 2 else 'dve',
                         (lambda g=g: nc.scalar.copy(out=KT[i][:, g * 512:(g + 1) * 512], in_=ptb[:, 0:512])) if g % 2 else
                         (lambda g=g: nc.vector.tensor_copy(out=KT[i][:, g * 512:(g + 1) * 512], in_=ptb[:, 0:512])),
                         r=['ps%d' % pt], w=['KT%d' % i])
                b.dma('pool', VV[i][:], cv[j][:, h * 128:(h + 1) * 128].rearrange("(t p) d -> p t d", p=128), w=['VV%d' % i])
                pO = b.ps(True); pD = b.ps(True)
                qT = Bq[:, h, 1024:1056]
                def emitSs(t):
                    pS_ = b.ps()
                    if t < 32:
                        b.op('pe', lambda: nc.tensor.matmul(PS[pS_][:, 0:32], KT[i][:, t * 128:(t + 1) * 128], qT, start=True, stop=True),
                             r=['KT%d' % i, 'Bq'], w=['ps%d' % pS_])
                    else:
                        b.op('pe', lambda: nc.tensor.matmul(PS[pS_][0:32, 0:32], KTs[:, h, :], qT, start=True, stop=True), r=['KTs', 'Bq'], w=['ps%d' % pS_])
                    return pS_
                pS_next = emitSs(0)
                for t in range(33):
                    last = t == 32
                    pS = pS_next
                    if not last:
                        pS_next = emitSs(t + 1)
                    pt_, ptk = tb()
                    if t < 32:
                        b.op('act', lambda t=t, h=h: nc.scalar.activation(out=pt_[:, 0:32], in_=PS[pS][:, 0:32], func=AF.Exp, bias=cup[:, t, h:h + 1], scale=1.0),
                             r=['ps%d' % pS, 'cup'], w=[ptk])
                        b.op('pe', lambda t=t: nc.tensor.matmul(PS[pO][:, 0:32], VV[i][:, t, :], pt_[:, 0:32], start=(t == 0), stop=False),
                             r=['VV%d' % i, ptk], w=['ps%d' % pO])
                        b.op('pe', lambda t=t: nc.tensor.matmul(PS[pD][:, 0:32], onesb[:, :], pt_[:, 0:32], start=(t == 0), stop=False),
                             r=['onesb', ptk], w=['ps%d' % pD])
                    else:
                        b.op('act', lambda h=h: nc.scalar.activation(out=pt_[0:32, 0:32], in_=PS[pS][0:32, 0:32], func=AF.Exp, bias=cum[0:32, 8, h:h + 1], scale=1.0),
                             r=['ps%d' % pS, 'fcum'], w=[ptk])
                        b.op('dve', lambda: nc.vector.tensor_tensor(out=pt_[0:32, 0:32], in0=pt_[0:32, 0:32], in1=maskb[0:32, 0, 0:32], op=ALU.mult),
                             r=[ptk, 'maskb'], w=[ptk])
                        b.op('pe', lambda h=h: nc.tensor.matmul(PS[pO][:, 0:32], Vs[0:32, h * 128:(h + 1) * 128], pt_[0:32, 0:32], start=False, stop=True),
                             r=['Vs', ptk], w=['ps%d' % pO])
                        b.op('pe', lambda: nc.tensor.matmul(PS[pD][:, 0:32], onesb[0:32, :], pt_[0:32, 0:32], start=False, stop=True),
                             r=['onesb', ptk], w=['ps%d' % pD])
                rd, rdk = tf()
                b.op('dve', lambda: nc.vector.reciprocal(out=rd[:, 0:32], in_=PS[pD][:, 0:32]), r=['ps%d' % pD], w=[rdk])
                b.op('dve', lambda h=h: nc.vector.tensor_tensor(out=A[:, h, 1024:1056], in0=PS[pO][:, 0:32], in1=rd[:, 0:32], op=ALU.mult),
                     r=['ps%d' % pO, rdk], w=['A'])
                b.rel(pO, pD)

            b.barrier()
            ses.close()
            stage('fox_sample')
            CR = b.sb(b.key("CR"), [128, 3, 8, 16], F32, les)
            CT = b.sb(b.key("CT"), [128, 4, 16], F32, les)
            DR = b.sb(b.key("DR"), [128, 3, 16], F32, les)
            BL = b.sb(b.key("BL"), [128, 8, 16], F32, les)
            BR = b.sb(b.key("BR"), [128, 3, 8, 16], F32, les)
            for r_ in range(3):
                b.dma('sp', CR[:, r_, :, :], ci_out.ap()[r_ * 1152:r_ * 1152 + 1024, :].rearrange("(t p) h -> p t h", p=128), r=['ci_out'], w=['CR'])
            for r_ in range(4):
                b.dma('sp', CT[:, r_, :], ci_out.ap()[r_ * 1152 + 1024:(r_ + 1) * 1152, :], r=['ci_out'], w=['CT'])
            for r_ in (2, 1, 0):
                b.op('dve', lambda r_=r_: nc.vector.tensor_scalar(out=DR[:, r_, :], in0=CT[:, r_, :], scalar1=LT[:, r_:r_ + 1], scalar2=-1.0,
                                                                 op0=ALU.mult, op1=ALU.mult), r=['CT', 'LT'], w=['DR'])
                if r_ < 2:
                    b.op('dve', lambda r_=r_: nc.vector.tensor_tensor(out=DR[:, r_, :], in0=DR[:, r_, :], in1=DR[:, r_ + 1, :], op=ALU.add), r=['DR'], w=['DR'])
            mbt = b.sb(b.key("mbt"), [128, 4], F32, les)
            b.op('dve', lambda: nc.vector.tensor_scalar(out=mbt[:], in0=LT[:], scalar1=-1.0, scalar2=-NEG, op0=ALU.add, op1=ALU.mult), r=['LT'], w=['mbt'])
            DRm = b.sb(b.key("DRm"), [128, 3, 16], F32, les)
            for r_ in range(3):
                b.op('dve', lambda r_=r_: nc.vector.tensor_scalar(out=DRm[:, r_, :], in0=DR[:, r_, :], scalar1=mbt[:, r_:r_ + 1], scalar2=None, op0=ALU.add),
                     r=['DR', 'mbt'], w=['DRm'])
            hn = [0]
            for Q in range(2):
                q0 = 512 * Q
                refN = pre[:, 4 * Q + 2, :]
                for t in range(4 * Q + 4):
                    b.op('dve', lambda t=t: nc.vector.tensor_tensor(out=BL[:, t, :], in0=cum[:, t, :], in1=refN, op=ALU.subtract), r=['fcum', 'fpre'], w=['BL'])
                for r_ in range(3):
                    for t in range(8):
                        b.op('dve', lambda t=t, r_=r_: nc.vector.tensor_tensor(out=BR[:, r_, t, :], in0=CR[:, r_, t, :], in1=DRm[:, r_, :], op=ALU.add),
                             r=['CR', 'DRm'], w=['BR'])
                        b.op('dve', lambda t=t, r_=r_: nc.vector.tensor_tensor(out=BR[:, r_, t, :], in0=BR[:, r_, t, :], in1=refN, op=ALU.subtract),
                             r=['BR', 'fpre'], w=['BR'])
                for h in range(16):
                    i = hn[0]; hn[0] = (i + 1) % 2
                    hc, hl = h // 2, h % 2
                    b.dma('sp', KT[i][:, 0:3072].rearrange("p (r t) -> p r t", r=3),
                          kt_out[hc].ap().rearrange("(r c) t -> c r t", r=4)[hl * 128:(hl + 1) * 128, 0:3, :], r=['kt_out'], w=['KT%d' % i])
                    b.dma('sp', KT[i][:, 3072:4096], kt_in[hc].ap()[hl * 128:(hl + 1) * 128, :], r=['kt_in'], w=['KT%d' % i])
                    for t in range(8):
                        b.dma('sp', VV[i][:, 0:24, :].rearrange("p (r t) d -> p r t d", r=3)[:, :, t, :],
                              v_out[t].ap().rearrange("(r p) c -> p r c", p=128)[:, 0:3, h * 128:(h + 1) * 128], r=['v_out'], w=['VV%d' % i])
                        b.dma('sp', VV[i][:, 24 + t, :], v_in[t].ap()[:, h * 128:(h + 1) * 128], r=['v_in'], w=['VV%d' % i])
                    qT = Bq[:, h, q0:q0 + 512]
                    pO = b.ps(True); pD = b.ps(True)
                    tiles = [(r_ * 8 + t, BR[:, r_, t, h:h + 1], -1) for r_ in range(3) for t in range(8)]
                    tiles += [(24 + t, BL[:, t, h:h + 1], t - 4 * Q) for t in range(4 * Q + 4)]
                    def emitS(kt_):
                        pS_ = b.ps()
                        b.op('pe', lambda: nc.tensor.matmul(PS[pS_][:, 0:512], KT[i][:, kt_ * 128:(kt_ + 1) * 128], qT, start=True, stop=True),
                             r=['KT%d' % i, 'Bq'], w=['ps%d' % pS_])
                        return pS_
                    pS_next = emitS(tiles[0][0])
                    for idx, (kt_, bias_ap, mi) in enumerate(tiles):
                        last = idx == len(tiles) - 1
                        pS = pS_next
                        if not last:
                            pS_next = emitS(tiles[idx + 1][0])
                        pt_, ptk = tb()
                        b.op('act', lambda bias_ap=bias_ap: nc.scalar.activation(out=pt_[:, :], in_=PS[pS][:, :], func=AF.Exp, bias=bias_ap, scale=1.0),
                             r=['ps%d' % pS, 'BL', 'BR'], w=[ptk])
                        if mi >= 0:
                            b.op('dve', lambda mi=mi: nc.vector.tensor_tensor(out=pt_[:, :], in0=pt_[:, :], in1=maskb[:, mi, :], op=ALU.mult),
                                 r=[ptk, 'maskb'], w=[ptk])
                        b.op('pe', lambda kt_=kt_, idx=idx: nc.tensor.matmul(PS[pO][:, :], VV[i][:, kt_, :], pt_[:, :], start=(idx == 0), stop=last),
                             r=['VV%d' % i, ptk], w=['ps%d' % pO])
                        b.op('pe', lambda idx=idx: nc.tensor.matmul(PS[pD][:, :], onesb[:, :], pt_[:, :], start=(idx == 0), stop=last),
                             r=['onesb', ptk], w=['ps%d' % pD])
                    rd, rdk = tf()
                    b.op('dve', lambda: nc.vector.reciprocal(out=rd[:, :], in_=PS[pD][:, :]), r=['ps%d' % pD], w=[rdk])
                    b.op('dve', lambda h=h: nc.vector.tensor_tensor(out=A[:, h, q0:q0 + 512], in0=PS[pO][:, :], in1=rd[:, :], op=ALU.mult),
                         r=['ps%d' % pO, rdk], w=['A'])
                    b.rel(pO, pD)

            stage('fox_prompt')
            rmsnorm(l * 4 + 1, Bq, 'Bq')

            def epi_o(ci, t0, n, pi):
                sg, sgk = tb()
                b.op('act', lambda: nc.scalar.activation(out=sg[:, 0:n], in_=PS[pi][:, 0:n], func=AF.Sigmoid), r=['ps%d' % pi], w=[sgk])
                b.op('dve', lambda: nc.vector.tensor_tensor(out=A[:, ci, t0:t0 + n], in0=A[:, ci, t0:t0 + n], in1=sg[:, 0:n], op=ALU.mult),
                     r=[sgk, 'A'], w=['A'])
            gemm_fm(W, [3 * D + c * 128 for c in range(16)], Bq, 'Bq', epi_o)

            def epi_out(ci, t0, n, pi):
                b.op('dve', lambda: nc.vector.tensor_tensor(out=xT[:, ci, t0:t0 + n], in0=PS[pi][:, 0:n], in1=xT[:, ci, t0:t0 + n], op=ALU.add),
                     r=['ps%d' % pi, 'xT'], w=['xT'])
            gemm_fm(f_wout[j], [c * 128 for c in range(16)], A, 'A', epi_out)
            b.barrier()

    def store_y():
        with ExitStack() as les:
            st = [b.sb("yst%d" % i, [128, D], F32, les) for i in range(2)]
            for ti, (t0, n) in enumerate(TTL):
                sk = 'yst%d' % (ti % 2)
                for g in range(4):
                    pi = b.ps()
                    for q in range(4):
                        kc = g * 4 + q
                        b.op('pe', lambda kc=kc, q=q: nc.tensor.transpose(PS[pi][0:n, q * 128:(q + 1) * 128], xT[:, kc, t0:t0 + n], ident[:, :]),
                             r=['xT', 'ident'], w=['ps%d' % pi])
                    b.op('dve' if g % 2 == 0 else 'act',
                         (lambda g=g: nc.vector.tensor_copy(out=st[ti % 2][0:n, g * 512:(g + 1) * 512], in_=PS[pi][0:n, :])) if g % 2 == 0 else
                         (lambda g=g: nc.scalar.copy(out=st[ti % 2][0:n, g * 512:(g + 1) * 512], in_=PS[pi][0:n, :])),
                         r=['ps%d' % pi], w=[sk])
                b.dma('sp', y[t0:t0 + n, :], st[ti % 2][0:n, :], r=[sk], w=['y'])
            b.barrier()

    try:
        stage('loadx')
        for l in range(DEPTH):
            ffn(l, 0, l * 4 + 0)
            b.barrier()
            stage('ffn1_%d' % l)
            if l % 2 == 0:
                mlstm(l, l // 2)
            else:
                fox(l, l // 2)
            stage('mix_%d' % l)
            ffn(l, 1, l * 4 + 2)
            b.barrier()
            ple(l)
            stage('layer_%d' % l)
    except StopBuild:
        pass
    store_y()
    b.barrier()
    return b


_CACHE = {}
STOP = None
import os as _os
FOXV = int(_os.environ.get('FOXV', '0'))
NO_CC = False
SKIP_FFN = False


def kernel(**inp):
    f = lambda a: np.ascontiguousarray(np.asarray(a, dtype=np.float32))
    if STOP not in _CACHE:
        _CACHE[STOP] = build(STOP)
    bld = _CACHE[STOP]
    shared = {
        "norm_gains": f(inp['norm_gains']).reshape(DEPTH * 4 * KC, 128),
        "ffn1_in": f(inp['ffn1_in']), "ffn2_in": f(inp['ffn2_in']),
        "ffn1_out": f(inp['ffn1_out']), "ffn2_out": f(inp['ffn2_out']),
        "ple_gate": f(inp['ple_gate']), "ple_proj": f(inp['ple_proj']),
        "mlstm_w_in": f(inp['mlstm_w_in']), "mlstm_b_gates": f(inp['mlstm_b_gates']).reshape(2, 16),
        "mlstm_g_h": f(inp['mlstm_g_h']).reshape(2 * KC, 128), "mlstm_w_out": f(inp['mlstm_w_out']),
        "fox_w_in": f(inp['fox_w_in']), "fox_b_f": f(inp['fox_b_f']),
        "fox_g_qk": f(inp['fox_g_qk']).reshape(4, 128), "fox_w_out": f(inp['fox_w_out']),
    }
    xp = f(inp['x_prompt']); xs = f(inp['x_sample']); pp = f(inp['p_prompt']); psm = f(inp['p_sample'])
    sCf = f(inp['state_mlstm_C']); snf = f(inp['state_mlstm_n']); smf = f(inp['state_mlstm_m'])
    ckf = f(inp['cache_fox_k']); cvf = f(inp['cache_fox_v']); clff = f(inp['cache_fox_lf'])
    in_maps = []
    for c in range(8):
        bb, r = c // 4, c % 4
        m = dict(shared)
        m["xin"] = np.concatenate([xp[bb, r * NPR:(r + 1) * NPR], xs[c]], axis=0)
        m["pin"] = np.concatenate([pp[:, bb, r * NPR:(r + 1) * NPR], psm[:, c]], axis=1)
        m["sC"] = np.ascontiguousarray(sCf[:, c]); m["sn"] = np.ascontiguousarray(snf[:, c]); m["sm"] = np.ascontiguousarray(smf[:, c])
        if 'ck' in bld.used:
            m["ck"] = np.ascontiguousarray(ckf[:, c].reshape(2, PAST, D)); m["cv"] = np.ascontiguousarray(cvf[:, c].reshape(2, PAST, D))
            m["clf"] = np.ascontiguousarray(clff[:, c])
        lt = np.zeros((1, 4), np.float32); lt[0, :r] = 1.0
        m["lt"] = lt
        m = {k: v for k, v in m.items() if k in bld.used}
        in_maps.append(m)
    res = run_bass_kernel_spmd(bld.nc, in_maps, core_ids=list(range(8))).results
    y_p = np.zeros((2, 4096, D), np.float32); y_s = np.zeros((8, NSM, D), np.float32)
    C_p = np.zeros((2, 2, 8, 128, 256), np.float32); n_p = np.zeros((2, 2, 8, 128), np.float32); m_p = np.zeros((2, 2, 8), np.float32)
    C_s = np.zeros((2, 8, 8, 128, 256), np.float32); n_s = np.zeros((2, 8, 8, 128), np.float32); m_s = np.zeros((2, 8, 8), np.float32)
    k_p = np.zeros((2, 2, 4096, 16, 128), np.float32); v_p = np.zeros_like(k_p); lf_p = np.zeros((2, 2, 4096, 16), np.float32)
    k_s = np.zeros((2, 8, NSM, 16, 128), np.float32); v_s = np.zeros_like(k_s); lf_s = np.zeros((2, 8, NSM, 16), np.float32)
    for c in range(8):
        bb, r = c // 4, c % 4
        o = res[c]
        sl = slice(r * NPR, (r + 1) * NPR)
        y_p[bb, sl] = o["y"][0:NPR]; y_s[c] = o["y"][NPR:TOK]
        for jj in range(2):
            if r == 3:
                C_p[jj, bb] = o["Cst"][jj, 0]; n_p[jj, bb] = o["nst"][jj, 0].T; m_p[jj, bb] = o["mst"][jj, 0, 0]
            C_s[jj, c] = o["Cst"][jj, 1]; n_s[jj, c] = o["nst"][jj, 1].T; m_s[jj, c] = o["mst"][jj, 1, 0]
            k_p[jj, bb, sl] = o["kout"][jj, 0:NPR].reshape(NPR, 16, 128); k_s[jj, c] = o["kout"][jj, NPR:TOK].reshape(NSM, 16, 128)
            v_p[jj, bb, sl] = o["vout"][jj, 0:NPR].reshape(NPR, 16, 128); v_s[jj, c] = o["vout"][jj, NPR:TOK].reshape(NSM, 16, 128)
            lf_p[jj, bb, sl] = o["lfout"][jj, 0:NPR]; lf_s[jj, c] = o["lfout"][jj, NPR:TOK]
    return (y_p, y_s, C_p, n_p, m_p, k_p, v_p, lf_p, C_s, n_s, m_s, k_s, v_s, lf_s)
```
